# Optimizing a Trainium2 kernel written in Bass

```python
import math
import jax
import jax.numpy as jnp
from jax import lax
import numpy as np

D_MODEL = 2048
BATCH = 4
SEQ = 2048
DEPTH = 2

GRID_W = 64
CTX_LEN = 256

D_SSM = D_MODEL // 2
SSM_HEAD_DIM = 64
SSM_HEADS = D_SSM // SSM_HEAD_DIM
SSM_GROUPS = 4
SSM_STATE = 128
SSM_CONV = 5
SSM_CHUNK = 128
D_XBC = D_SSM + 2 * SSM_GROUPS * SSM_STATE

DIFF_HEAD_DIM = 64
DIFF_HEADS = (D_MODEL // 2) // (2 * DIFF_HEAD_DIM)
D_QK_DIFF = 2 * DIFF_HEADS * DIFF_HEAD_DIM
D_V_DIFF = DIFF_HEADS * 2 * DIFF_HEAD_DIM

D_MIX_EVEN = D_SSM + D_V_DIFF
IN_EVEN = D_SSM + D_XBC + 2 * SSM_HEADS + 2 * D_QK_DIFF + D_V_DIFF

NA_HEAD_DIM = 128
NA_HEADS = D_MODEL // NA_HEAD_DIM
D_NA = NA_HEADS * NA_HEAD_DIM
WIN_ROWS = 8
WIN_COLS = 16

D_FF = ((8 * D_MODEL // 3 + 255) // 256) * 256

N_EVEN = (DEPTH + 1) // 2
N_ODD = DEPTH // 2

QUERY_BLOCK = 128
ROPE_BASE = 10000.0
NORM_EPS = 1e-6

kernel_name = 'hybrid_ssd_diffattn_natten_prefix_dit'


def rms_norm(x, g):
    xf = x.astype(jnp.float32)
    xf = xf * lax.rsqrt(jnp.mean(xf * xf, axis=-1, keepdims=True) + NORM_EPS)
    return (xf * g.astype(jnp.float32)).astype(x.dtype)


def modulate(h, shift, scale):
    return h * (1.0 + scale) + shift


def swiglu(h, w1, w3, w2):
    return (jax.nn.silu(h @ w1) * (h @ w3)) @ w2


def axial_rope(x, row, col):
    half = x.shape[-1] // 2
    inv_freq = ROPE_BASE ** (-jnp.arange(0, half, 2, dtype=jnp.float32) / half)

    def rot(u, p):
        ang = p.astype(jnp.float32)[:, None] * inv_freq[None, :]
        cos = jnp.cos(ang)[None, :, None, :].astype(u.dtype)
        sin = jnp.sin(ang)[None, :, None, :].astype(u.dtype)
        u1, u2 = jnp.split(u, 2, axis=-1)
        return jnp.concatenate([u1 * cos - u2 * sin, u1 * sin + u2 * cos], axis=-1)

    xr, xc = jnp.split(x, 2, axis=-1)
    return jnp.concatenate([rot(xr, row), rot(xc, col)], axis=-1)


def depthwise_conv_centred(u, w, bias):
    k_w = w.shape[0]
    n = u.shape[1]
    up = jnp.pad(u, ((0, 0), (k_w // 2, k_w // 2), (0, 0)))
    return sum(up[:, t:t + n] * w[t] for t in range(k_w)) + bias


def ssd_chunked_scan(xs, dt, a, bm, cm, h0):
    b, n, nh, hp = xs.shape
    ng, ns = bm.shape[2], bm.shape[3]
    nc = n // SSM_CHUNK
    rep = nh // ng
    bh = jnp.repeat(bm, rep, axis=2).reshape(b, nc, SSM_CHUNK, nh, ns)
    ch = jnp.repeat(cm, rep, axis=2).reshape(b, nc, SSM_CHUNK, nh, ns)
    xdt = (xs * dt[..., None]).reshape(b, nc, SSM_CHUNK, nh, hp)
    a_cum = jnp.cumsum((dt * a).astype(jnp.float32).reshape(b, nc, SSM_CHUNK, nh), axis=2)

    lower = jnp.tril(jnp.ones((SSM_CHUNK, SSM_CHUNK), dtype=bool))
    seg = a_cum[:, :, :, None, :] - a_cum[:, :, None, :, :]
    decay_ls = jnp.exp(jnp.where(lower[None, None, :, :, None], seg, -jnp.inf)).astype(xs.dtype)
    cb = jnp.einsum('bclhn,bcshn->bclsh', ch, bh) * decay_ls
    y_diag = jnp.einsum('bclsh,bcshp->bclhp', cb, xdt)

    decay_to_end = jnp.exp(a_cum[:, :, -1:, :] - a_cum).astype(xs.dtype)
    chunk_states = jnp.einsum('bclhn,bclhp->bchpn', bh * decay_to_end[..., None], xdt)
    chunk_decay = jnp.exp(a_cum[:, :, -1, :]).astype(xs.dtype)

    def carry_state(h, inp):
        s_c, d_c = inp
        return h * d_c[:, :, None, None] + s_c, h

    h_final, h_start = lax.scan(carry_state, h0,
                                (jnp.moveaxis(chunk_states, 1, 0), jnp.moveaxis(chunk_decay, 1, 0)))
    h_start = jnp.moveaxis(h_start, 0, 1)
    y_off = jnp.einsum('bclhn,bchpn->bclhp', ch * jnp.exp(a_cum).astype(xs.dtype)[..., None], h_start)
    return (y_diag + y_off).reshape(b, n, nh, hp), h_final


def ssd_bidirectional(xs, dt_f, dt_b, a_f, a_b, bm, cm, h0_f, h0_b):
    flip = lambda t: jnp.flip(t, axis=1)
    y_f, h_f = ssd_chunked_scan(xs, dt_f, a_f, bm, cm, h0_f)
    y_b, h_b = ssd_chunked_scan(flip(xs), flip(dt_b), a_b, flip(bm), flip(cm), h0_b)
    return y_f + flip(y_b), h_f, h_b


def diff_attend(q, k, v, lam):
    b, nq, h2, d = q.shape
    s = jnp.einsum('bqhd,bkhd->bhqk', q, k).astype(jnp.float32) * (d ** -0.5)
    p = jax.nn.softmax(s, axis=-1).reshape(b, h2 // 2, 2, nq, k.shape[1])
    pd = (p[:, :, 0] - lam * p[:, :, 1]).astype(v.dtype)
    return jnp.einsum('bhqk,bkhe->bqhe', pd, v)


def softmax_attention(q, k, v):
    s = jnp.einsum('bqhd,bkhd->bhqk', q, k).astype(jnp.float32) * (q.shape[-1] ** -0.5)
    p = jax.nn.softmax(s, axis=-1).astype(v.dtype)
    return jnp.einsum('bhqk,bkhd->bqhd', p, v)


def blocked_queries(fn, q):
    b, n = q.shape[:2]
    nb = n // QUERY_BLOCK
    qb = jnp.moveaxis(q.reshape(b, nb, QUERY_BLOCK, *q.shape[2:]), 1, 0)
    out = lax.map(fn, qb)
    return jnp.moveaxis(out, 0, 1).reshape(b, n, *out.shape[3:])


def neighbourhood_attention(q, k, v, k_ctx, v_ctx, rpb):
    b, n, nh, hd = q.shape
    rows = n // GRID_W
    kr = min(WIN_ROWS, rows)
    nw = kr * GRID_W
    scale = hd ** -0.5
    qg = q.reshape(b, rows, GRID_W, nh, hd)
    kg = k.reshape(b, rows, GRID_W, nh, hd)
    vg = v.reshape(b, rows, GRID_W, nh, hd)
    qcol = jnp.arange(GRID_W)
    kcol = jnp.arange(GRID_W)
    col_start = jnp.clip(qcol - WIN_COLS // 2, 0, GRID_W - WIN_COLS)
    col_ok = (kcol[None, :] >= col_start[:, None]) & (kcol[None, :] < col_start[:, None] + WIN_COLS)
    mask = jnp.broadcast_to(col_ok[:, None, :], (GRID_W, kr, GRID_W)).reshape(GRID_W, nw)
    dc_idx = jnp.clip(kcol[None, :] - qcol[:, None] + WIN_COLS - 1, 0, 2 * WIN_COLS - 2)
    rpb_cols = rpb[:, :, dc_idx]

    def row_block(r):
        rs = jnp.clip(r - kr // 2, 0, rows - kr)
        q_r = lax.dynamic_index_in_dim(qg, r, axis=1, keepdims=False)
        k_r = lax.dynamic_slice_in_dim(kg, rs, kr, axis=1).reshape(b, nw, nh, hd)
        v_r = lax.dynamic_slice_in_dim(vg, rs, kr, axis=1).reshape(b, nw, nh, hd)
        dr_idx = rs + jnp.arange(kr) - r + WIN_ROWS - 1
        bias = jnp.moveaxis(rpb_cols[:, dr_idx], 1, 2).reshape(nh, GRID_W, nw)
        s_win = jnp.einsum('bqhd,bkhd->bhqk', q_r, k_r).astype(jnp.float32) * scale + bias.astype(jnp.float32)
        s_win = jnp.where(mask, s_win, -jnp.inf)
        s_ctx = jnp.einsum('bqhd,bkhd->bhqk', q_r, k_ctx).astype(jnp.float32) * scale
        p = jax.nn.softmax(jnp.concatenate([s_win, s_ctx], axis=-1), axis=-1).astype(v.dtype)
        return (jnp.einsum('bhqk,bkhd->bqhd', p[..., :nw], v_r)
                + jnp.einsum('bhqk,bkhd->bqhd', p[..., nw:], v_ctx))

    out = lax.map(row_block, jnp.arange(rows))
    return jnp.moveaxis(out, 0, 1).reshape(b, n, nh, hd)


def even_mixer(h_lat, h_ctx, row, col, w_in, conv_w, conv_b, a_log, dt_bias, d_skip, ssm_norm_g,
               lam_q1, lam_k1, lam_q2, lam_k2, subln_g, w_out, lambda_init, with_ctx_out):
    b = h_lat.shape[0]
    s0 = D_SSM
    s1 = s0 + D_XBC
    s2 = s1 + SSM_HEADS
    s3 = s2 + SSM_HEADS
    s4 = s3 + D_QK_DIFF
    s5 = s4 + D_QK_DIFF
    splits = [s0, s1, s2, s3, s4, s5]
    z_l, xbc_l, dtf_l, dtb_l, q_l, k_l, v_l = jnp.split(h_lat @ w_in, splits, axis=-1)
    z_c, xbc_c, dtf_c, dtb_c, q_c, k_c, v_c = jnp.split(h_ctx @ w_in, splits, axis=-1)

    decay = -jnp.exp(a_log)

    def ssd_prep(xbc, dtf, dtb):
        n = xbc.shape[1]
        xbc = jax.nn.silu(depthwise_conv_centred(xbc, conv_w, conv_b))
        xs, bm, cm = jnp.split(xbc, [D_SSM, D_SSM + SSM_GROUPS * SSM_STATE], axis=-1)
        return (xs.reshape(b, n, SSM_HEADS, SSM_HEAD_DIM),
                jax.nn.softplus(dtf + dt_bias[0]), jax.nn.softplus(dtb + dt_bias[1]),
                bm.reshape(b, n, SSM_GROUPS, SSM_STATE), cm.reshape(b, n, SSM_GROUPS, SSM_STATE))

    def ssd_finish(y, xs, z):
        n = y.shape[1]
        y = (y + d_skip[:, None] * xs).reshape(b, n, D_SSM) * jax.nn.silu(z)
        y = rms_norm(y.reshape(b, n, SSM_GROUPS, D_SSM // SSM_GROUPS), ssm_norm_g.reshape(SSM_GROUPS, -1))
        return y.reshape(b, n, D_SSM)

    xs_c, dtf_c, dtb_c, bm_c, cm_c = ssd_prep(xbc_c, dtf_c, dtb_c)
    h0 = jnp.zeros((b, SSM_HEADS, SSM_HEAD_DIM, SSM_STATE), xs_c.dtype)
    y_c, hf_c, hb_c = ssd_bidirectional(xs_c, dtf_c, dtb_c, decay[0], decay[1], bm_c, cm_c, h0, h0)
    xs_l, dtf_l, dtb_l, bm_l, cm_l = ssd_prep(xbc_l, dtf_l, dtb_l)
    y_l, _, _ = ssd_bidirectional(xs_l, dtf_l, dtb_l, decay[0], decay[1], bm_l, cm_l, hf_c, hb_c)

    lam = (jnp.exp(jnp.sum(lam_q1.astype(jnp.float32) * lam_k1.astype(jnp.float32)))
           - jnp.exp(jnp.sum(lam_q2.astype(jnp.float32) * lam_k2.astype(jnp.float32))) + lambda_init)
    qk_heads = lambda t: t.reshape(t.shape[0], t.shape[1], 2 * DIFF_HEADS, DIFF_HEAD_DIM)
    v_heads = lambda t: t.reshape(t.shape[0], t.shape[1], DIFF_HEADS, 2 * DIFF_HEAD_DIM)
    kd_c, vd_c = qk_heads(k_c), v_heads(v_c)
    k_all = jnp.concatenate([kd_c, axial_rope(qk_heads(k_l), row, col)], axis=1)
    v_all = jnp.concatenate([vd_c, v_heads(v_l)], axis=1)
    o_l = blocked_queries(lambda qb: diff_attend(qb, k_all, v_all, lam), axial_rope(qk_heads(q_l), row, col))

    def diff_finish(o):
        return (rms_norm(o, subln_g) * (1.0 - lambda_init)).reshape(o.shape[0], o.shape[1], D_V_DIFF)

    out_lat = jnp.concatenate([ssd_finish(y_l, xs_l, z_l), diff_finish(o_l)], axis=-1) @ w_out
    if not with_ctx_out:
        return out_lat, None
    o_c = diff_attend(qk_heads(q_c), kd_c, vd_c, lam)
    out_ctx = jnp.concatenate([ssd_finish(y_c, xs_c, z_c), diff_finish(o_c)], axis=-1) @ w_out
    return out_lat, out_ctx


def odd_mixer(h_lat, h_ctx, w_in, rpb, w_out, with_ctx_out):
    b, n_lat = h_lat.shape[:2]
    n_ctx = h_ctx.shape[1]
    heads = lambda t: t.reshape(t.shape[0], t.shape[1], NA_HEADS, NA_HEAD_DIM)
    q_l, k_l, v_l = [heads(t) for t in jnp.split(h_lat @ w_in, 3, axis=-1)]
    k_c, v_c = [heads(t) for t in jnp.split(h_ctx @ w_in[:, D_NA:], 2, axis=-1)]
    out_lat = neighbourhood_attention(q_l, k_l, v_l, k_c, v_c, rpb).reshape(b, n_lat, D_NA) @ w_out
    if not with_ctx_out:
        return out_lat, None
    q_c = heads(h_ctx @ w_in[:, :D_NA])
    out_ctx = softmax_attention(q_c, k_c, v_c).reshape(b, n_ctx, D_NA) @ w_out
    return out_lat, out_ctx


def setup_inputs(seed: int = 0) -> dict:
    key = jax.random.key(seed)
    keys = iter(jax.random.split(key, 40))

    def normal(shape, scale):
        return scale * jax.random.normal(next(keys), shape, dtype=jnp.float32)

    def gain(shape):
        return 1.0 + normal(shape, 0.02)

    d = D_MODEL
    a_init = jax.random.uniform(next(keys), (N_EVEN, 2, SSM_HEADS), jnp.float32, 1.0, 16.0)
    dt0 = jnp.exp(jax.random.uniform(next(keys), (N_EVEN, 2, SSM_HEADS), jnp.float32,
                                     math.log(1e-3), math.log(1e-1)))
    return {
        'x': normal((BATCH, SEQ, d), 1.0),
        'c': normal((BATCH, d), 1.0),
        'ctx': normal((BATCH, CTX_LEN, d), 1.0),
        'c_ctx': normal((d,), 1.0),
        'ada_w': normal((DEPTH, d, 6 * d), 0.5 * d ** -0.5),
        'ada_b': normal((DEPTH, 6 * d), 0.02),
        'norm_mix_g': gain((DEPTH, d)),
        'norm_ffn_g': gain((DEPTH, d)),
        'final_norm_g': gain((d,)),
        'ffn_w1': normal((DEPTH, d, D_FF), d ** -0.5),
        'ffn_w3': normal((DEPTH, d, D_FF), d ** -0.5),
        'ffn_w2': normal((DEPTH, D_FF, d), D_FF ** -0.5),
        'ev_w_in': normal((N_EVEN, d, IN_EVEN), d ** -0.5),
        'ev_conv_w': normal((N_EVEN, SSM_CONV, D_XBC), SSM_CONV ** -0.5),
        'ev_conv_b': normal((N_EVEN, D_XBC), 0.02),
        'ev_a_log': jnp.log(a_init),
        'ev_dt_bias': dt0 + jnp.log(-jnp.expm1(-dt0)),
        'ev_d_skip': 1.0 + normal((N_EVEN, SSM_HEADS), 0.1),
        'ev_ssm_norm_g': gain((N_EVEN, D_SSM)),
        'ev_lam_q1': normal((N_EVEN, DIFF_HEAD_DIM), 0.1),
        'ev_lam_k1': normal((N_EVEN, DIFF_HEAD_DIM), 0.1),
        'ev_lam_q2': normal((N_EVEN, DIFF_HEAD_DIM), 0.1),
        'ev_lam_k2': normal((N_EVEN, DIFF_HEAD_DIM), 0.1),
        'ev_subln_g': gain((N_EVEN, 2 * DIFF_HEAD_DIM)),
        'ev_w_out': normal((N_EVEN, D_MIX_EVEN, d), D_MIX_EVEN ** -0.5),
        'od_w_in': normal((N_ODD, d, 3 * D_NA), d ** -0.5),
        'od_rpb': normal((N_ODD, NA_HEADS, 2 * WIN_ROWS - 1, 2 * WIN_COLS - 1), 0.1),
        'od_w_out': normal((N_ODD, D_NA, d), D_NA ** -0.5),
    }


def reference(x, c, ctx, c_ctx, ada_w, ada_b, norm_mix_g, norm_ffn_g, final_norm_g,
              ffn_w1, ffn_w3, ffn_w2, ev_w_in, ev_conv_w, ev_conv_b, ev_a_log, ev_dt_bias,
              ev_d_skip, ev_ssm_norm_g, ev_lam_q1, ev_lam_k1, ev_lam_q2, ev_lam_k2, ev_subln_g,
              ev_w_out, od_w_in, od_rpb, od_w_out):
    n_tok = x.shape[1]
    pos = jnp.arange(n_tok)
    row, col = pos // GRID_W, pos % GRID_W
    cond_lat = jax.nn.silu(c)
    cond_ctx = jax.nn.silu(c_ctx)
    for i in range(DEPTH):
        ctx_out = i < DEPTH - 1
        sh1, sc1, g1, sh2, sc2, g2 = jnp.split((cond_lat @ ada_w[i] + ada_b[i])[:, None, :], 6, axis=-1)
        csh1, csc1, cg1, csh2, csc2, cg2 = jnp.split(cond_ctx @ ada_w[i] + ada_b[i], 6, axis=-1)
        h_lat = modulate(rms_norm(x, norm_mix_g[i]), sh1, sc1)
        h_ctx = modulate(rms_norm(ctx, norm_mix_g[i]), csh1, csc1)
        j = i // 2
        if i % 2 == 0:
            lambda_init = 0.8 - 0.6 * math.exp(-0.3 * i)
            o_lat, o_ctx = even_mixer(h_lat, h_ctx, row, col, ev_w_in[j], ev_conv_w[j], ev_conv_b[j],
                                      ev_a_log[j], ev_dt_bias[j], ev_d_skip[j], ev_ssm_norm_g[j],
                                      ev_lam_q1[j], ev_lam_k1[j], ev_lam_q2[j], ev_lam_k2[j],
                                      ev_subln_g[j], ev_w_out[j], lambda_init, ctx_out)
        else:
            o_lat, o_ctx = odd_mixer(h_lat, h_ctx, od_w_in[j], od_rpb[j], od_w_out[j], ctx_out)
        x = x + g1 * o_lat
        x = x + g2 * swiglu(modulate(rms_norm(x, norm_ffn_g[i]), sh2, sc2), ffn_w1[i], ffn_w3[i], ffn_w2[i])
        if ctx_out:
            ctx = ctx + cg1 * o_ctx
            ctx = ctx + cg2 * swiglu(modulate(rms_norm(ctx, norm_ffn_g[i]), csh2, csc2),
                                     ffn_w1[i], ffn_w3[i], ffn_w2[i])
    return rms_norm(x, final_norm_g)
```

```python
import math
from contextlib import ExitStack
import numpy as np
import concourse.bass as bass
import concourse.mybir as mybir
from concourse.bass_utils import run_bass_kernel_spmd

F32 = mybir.dt.float32
BF16 = mybir.dt.bfloat16
AF = mybir.ActivationFunctionType
ALU = mybir.AluOpType
AX = mybir.AxisListType

SAME_ENGINE_SYNC = True
import os
NCUT = int(os.environ.get('NCUT', '9'))
NVAR = os.environ.get('NVAR', '')
D = 2048
NLAT = 2048
NCTX = 256
NOWN = 1280
DFF = 5632
IN_EVEN = 6176
EPS = 1e-6
LAMBDA_INIT0 = 0.8 - 0.6 * math.exp(-0.3 * 0)


class Eng:
    def __init__(self, name, sem, is_pe=False):
        self.name = name
        self.sem = sem
        self.ops = []
        self.sig = 0
        self.seen = {}
        self.is_pe = is_pe


class Lane:
    def __init__(self, sem):
        self.sem = sem
        self.count = 0


class T:
    def __init__(self, name="", excl=False):
        self.name = name
        self.w = None
        self.r = {}
        self.excl = excl


class FW:
    def __init__(self, nc, stack):
        self.nc = nc
        self.stack = stack
        self.engs = {}
        for name in ("tensor", "vector", "scalar", "gpsimd", "sync"):
            sem = stack.enter_context(nc.semaphore("s_" + name))
            self.engs[name] = Eng(name, sem, is_pe=(name == "tensor"))
        self.owner = {e.sem: e for e in self.engs.values()}
        self.lanes = []
        self.n_ops = 0

    def lane(self, name):
        sem = self.stack.enter_context(self.nc.semaphore("l_" + name))
        ln = Lane(sem)
        self.lanes.append(ln)
        return ln

    def sbuf(self, name, shape, dtype, stack=None):
        return (stack or self.stack).enter_context(self.nc.sbuf_tensor("sb_" + name, list(shape), dtype))

    def psum(self, name, shape, dtype):
        return self.stack.enter_context(self.nc.psum_tensor("ps_" + name, list(shape), dtype))

    def _waits(self, eng, reads, writes, skip_self=False):
        waits = {}
        waits_r = {}

        def need(tok, raw=False):
            if tok is None:
                return
            sem, val = tok
            if waits.get(sem, 0) < val:
                waits[sem] = val
            if raw and waits_r.get(sem, 0) < val:
                waits_r[sem] = val

        for t in reads:
            need(t.w, raw=True)
            if t.excl:
                for sem_r, tok in t.r.items():
                    if self.owner.get(sem_r) is not eng:
                        need(tok)
        for t in writes:
            need(t.w)
            for tok in t.r.values():
                need(tok)
        for sem, val in waits.items():
            own = self.owner.get(sem)
            if own is eng:
                if eng.is_pe or not SAME_ENGINE_SYNC or eng.name == "sync":
                    continue
                if skip_self:
                    val = waits_r.get(sem, 0)
                    if val == 0:
                        continue
                if val > eng.sig:
                    continue
            elif own is not None:
                assert val <= own.sig, f"{eng.name} waits on unsignalled op of {own.name}: {val}>{own.sig}"
            if eng.seen.get(sem, 0) >= val:
                continue
            eng.seen[sem] = val
            eng.ops.append(("wait", sem, val))

    def op(self, engname, fn, reads=(), writes=(), signal=True, skip_self=False):
        eng = self.engs[engname]
        self._waits(eng, reads, writes, skip_self)
        if signal:
            eng.sig += 1
            tok = (eng.sem, eng.sig)
        else:
            tok = (eng.sem, eng.sig + 1)
        eng.ops.append(("op", fn, signal))
        for t in reads:
            t.r[eng.sem] = tok
        for t in writes:
            t.w = tok
            t.r = {}
        self.n_ops += 1

    def dma(self, engname, out, in_, lane, reads=(), writes=(), **kw):
        eng = self.engs[engname]
        self._waits(eng, reads, writes)
        if lane.count and eng.seen.get(lane.sem, 0) < lane.count:
            eng.seen[lane.sem] = lane.count
            eng.ops.append(("wait", lane.sem, lane.count))
        lane.count += 16
        tok = (lane.sem, lane.count)
        eng.ops.append(("dma", out, in_, lane.sem, kw))
        for t in reads:
            t.r[lane.sem] = tok
        for t in writes:
            t.w = tok
            t.r = {}
        self.n_ops += 1

    def barrier(self):
        toks = []
        for name in ("tensor", "vector", "scalar"):
            eng = self.engs[name]
            t = T("bar_" + name)
            if name == "tensor":
                continue
            toks.append((eng, t))
        vals = {e.sem: e.sig for e in self.engs.values() if e.name in ("tensor", "vector", "scalar", "gpsimd")}
        for ln in self.lanes:
            vals[ln.sem] = ln.count
        for eng in self.engs.values():
            for sem, val in vals.items():
                if val == 0 or eng.seen.get(sem, 0) >= val:
                    continue
                if self.owner.get(sem) is eng:
                    continue
                eng.seen[sem] = val
                eng.ops.append(("wait", sem, val))

    def emit(self):
        nc = self.nc
        with nc.Block() as block:
            def mk(eng):
                def body(e):
                    for o in eng.ops:
                        if o[0] == "wait":
                            e.wait_ge(o[1], o[2])
                        elif o[0] == "op":
                            ins = o[1](e)
                            if o[2]:
                                ins.then_inc(eng.sem, 1)
                        else:
                            _, out, in_, sem, kw = o
                            e.dma_start(out=out, in_=in_, **kw).then_inc(sem, 16)
                return body
            block.tensor(mk(self.engs["tensor"]))
            block.vector(mk(self.engs["vector"]))
            block.scalar(mk(self.engs["scalar"]))
            block.gpsimd(mk(self.engs["gpsimd"]))
            block.sync(mk(self.engs["sync"]))


def build_program(debug=False, stop_after=None):
    nc = bass.Bass("TRN2", target_bir_lowering=False)

    _SPECS = {
        "x": ("x", [NLAT, D]),
        "ctx": ("ctx", [NCTX, D]),
        "cond": ("cond", [128, 16, 2]),
        "adaw": ("ada_w", [2, D, 6 * D]),
        "adab": ("ada_b", [2, 128, 96]),
        "gfm": ("gfm", [128, 5, 16]),
        "w1": ("ffn_w1", [2, D, DFF]),
        "w3": ("ffn_w3", [2, D, DFF]),
        "w2": ("ffn_w2", [2, DFF, D]),
        "win0": ("w_in0", [D, IN_EVEN]),
        "convw": ("convw", [128, 16, 5]),
        "convb": ("convb", [128, 16]),
        "vec": ("vec", [1, 80]),
        "ssmg": ("ssmg", [1, 1024]),
        "lamv": ("lamv", [1, 256]),
        "subg": ("subg", [128, 1]),
        "wout0": ("w_out0", [D, D]),
        "win1": ("w_in1", [D, 3 * D]),
        "wout1": ("w_out1", [D, D]),
        "cos": ("cosT", [128, NLAT]),
        "sin": ("sinS", [128, NLAT]),
        "const": ("consts", [128, 6, 128]),
        "nab": ("na_bias", [16, 128, 15, 64]),
        "nacm": ("na_cmask", [128, 64]),
        "naval": ("na_valid", [128, 160]),
    }

    class _Lazy:
        def __init__(self):
            self._c = {}

        def __getattr__(self, k):
            if k.startswith("_"):
                raise AttributeError(k)
            if k not in self._c:
                n, sh = _SPECS[k]
                self._c[k] = nc.dram_tensor(n, list(sh), F32, kind="ExternalInput").ap()
            return self._c[k]
    I = _Lazy()
    out_d = nc.dram_tensor("out", [1024, D], F32, kind="ExternalOutput").ap()
    mix_scr = nc.dram_tensor("mix_scr", [16, 128, 1536], BF16).ap()
    _DBG = {"hT": ([128, 16, 2304], BF16), "mix": ([16, 128, 1536], BF16), "x1": ([128, 16, 1536], F32),
            "x2": ([128, 16, 1536], F32), "mod": ([128, 96, 2], F32), "mix1": ([128, 16, 1024], BF16),
            "x3": ([128, 16, 1024], F32)}

    class _LazyDbg(dict):
        def __missing__(self, k):
            sh, dt = _DBG[k]
            self[k] = nc.dram_tensor("dbg_" + k, list(sh), dt, kind="ExternalOutput").ap()
            return self[k]
    dbg = _LazyDbg()

    with ExitStack() as st:
        fw = FW(nc, st)
        op = fw.op

        consts = fw.sbuf("consts", [128, 6, 128], F32)
        ident = consts[:, 0, :]
        LE, GE, GT, LT, PM = (consts[:, i, :] for i in range(1, 6))
        identb = fw.sbuf("identb", [128, 128], BF16)
        onesb = fw.sbuf("onesb", [128, 128], BF16)
        onesf = fw.sbuf("onesf", [128, 128], F32)
        epsc = fw.sbuf("epsc", [128, 1], F32)
        onec = fw.sbuf("onec", [128, 1], F32)
        gfm = fw.sbuf("gfm", [128, 5, 16], F32)
        cond = fw.sbuf("cond", [128, 16, 2], F32)
        condb = fw.sbuf("condb", [128, 16, 2], BF16)
        mods = [fw.sbuf(f"mod{l}", [128, 96, 2], F32) for l in range(2)]
        modAs = [fw.sbuf(f"modA{l}", [128, 2, 2, 16], F32) for l in range(2)]
        adabs = [fw.sbuf(f"adab{l}", [128, 96], F32) for l in range(2)]
        CUR = [0]
        t_const = T("const")
        t_mod = T("mod")
        l_c = fw.lane("c")
        fw.dma("sync", consts[:], I.const, l_c, writes=[t_const])
        fw.dma("sync", gfm[:], I.gfm, l_c, writes=[t_const])
        fw.dma("sync", cond[:], I.cond, l_c, writes=[t_const])
        op("vector", lambda e: e.memset(onesf[:], 1.0), writes=[t_const])
        op("vector", lambda e: e.memset(epsc[:], EPS), writes=[t_const])
        op("vector", lambda e: e.memset(onec[:], 1.0), writes=[t_const])
        op("vector", lambda e: e.tensor_copy(out=onesb[:], in_=onesf[:]), reads=[t_const], writes=[t_const])
        op("vector", lambda e: e.tensor_copy(out=identb[:], in_=ident), reads=[t_const], writes=[t_const])
        op("scalar", lambda e: e.activation(out=cond[:], in_=cond[:], func=AF.Silu), reads=[t_const], writes=[t_const])
        op("vector", lambda e: e.tensor_copy(out=condb[:], in_=cond[:]), reads=[t_const], writes=[t_const])

        if stop_after == "pre":
            fw.dma("sync", dbg["mod"][:, 0:16, :], cond[:], l_c, reads=[t_const])
            fw.barrier()
            fw.emit()
            return nc

        xh = {}
        t_x = [T(f"x{i}") for i in range(4)]

        def xblk(i):
            if i < 2:
                return xh["main"][:, :, i * 512:(i + 1) * 512], 512
            return xh["b"][:, :, (i - 2) * 256:(i - 1) * 256], 256
        OWN0 = [0, 512, 1024, 1280]

        NS = 4
        wsl = [fw.sbuf(f"wsl{i}", [128, 4096], BF16) for i in range(NS)]
        t_w = [T(f"w{i}") for i in range(NS)]
        l_w = [fw.lane(f"w{i}") for i in range(NS)]

        class WStream:
            def __init__(self, specs, slots=None):
                self.specs = specs
                self.i = 0
                self.loaded = 0
                self.wsl, self.t_w, self.l_w = slots if slots is not None else (wsl, t_w, l_w)
                self.ns = len(self.wsl)
                for _ in range(min(self.ns, len(specs))):
                    self._load()

            def _view(self, j):
                src, a_, b_ = self.specs[j]
                s_ = j % self.ns
                return self.wsl[s_][:, 0:a_ * b_].rearrange("p (a b) -> p a b", b=b_), s_

            def _load(self):
                j = self.loaded
                v, s_ = self._view(j)
                fw.dma("gpsimd", v, self.specs[j][0], self.l_w[s_], writes=[self.t_w[s_]])
                self.loaded += 1

            def get(self):
                v, s_ = self._view(self.i)
                return v, self.t_w[s_]

            def slab(self, j):
                assert self.i <= j < self.loaded, (self.i, j, self.loaded)
                v, s_ = self._view(j)
                return v, self.t_w[s_]

            def done(self):
                self.i += 1
                if self.loaded < len(self.specs):
                    self._load()

        pbank = [fw.psum(f"pb{i}", [128, 512], F32) for i in range(8)]
        t_pb = [T(f"pb{i}", excl=True) for i in range(8)]
        prr = [0, 0]
        pool_n0 = [6]

        pool_end = [8]

        def set_pools(n0, end=8):
            pool_n0[0] = n0
            pool_end[0] = end

        def pget(pool=0):
            n0 = pool_n0[0]
            if pool == 0:
                i = prr[0] % n0
                prr[0] += 1
            else:
                i = n0 + prr[1] % (pool_end[0] - n0)
                prr[1] += 1
            return pbank[i], t_pb[i]

        l_io = [fw.lane(f"io{i}") for i in range(4)]
        io_rr = [0]

        def iolane():
            io_rr[0] += 1
            return l_io[io_rr[0] % 4]

        def mm(out, lhsT, rhs, start, stop, reads, writes, signal=None):
            op("tensor", lambda e: e.matmul(out, lhsT, rhs, start=start, stop=stop),
               reads=reads, writes=writes, signal=(stop if signal is None else signal))

        def tr(out, in_, idn, reads, writes, signal=True):
            op("tensor", lambda e: e.matmul(out, in_, idn, start=True, stop=True), reads=reads, writes=writes,
               signal=signal)

        def act(out, in_, func, reads, writes, **kw):
            op("scalar", lambda e: e.activation(out=out, in_=in_, func=func, **kw), reads=reads, writes=writes)

        def vtt(out, in0, in1, alu, reads, writes, eng="vector", skip_self=False):
            op(eng, lambda e: e.tensor_tensor(out=out, in0=in0, in1=in1, op=alu), reads=reads, writes=writes,
               skip_self=skip_self)

        def vts(out, in0, s1, s2, op0, op1, reads, writes, eng="vector"):
            if s2 is None:
                op(eng, lambda e: e.tensor_scalar(out=out, in0=in0, scalar1=s1, scalar2=None, op0=op0),
                   reads=reads, writes=writes)
            else:
                op(eng, lambda e: e.tensor_scalar(out=out, in0=in0, scalar1=s1, scalar2=s2, op0=op0, op1=op1),
                   reads=reads, writes=writes)

        def vstt(out, in0, scalar, in1, op0, op1, reads, writes, eng="vector", skip_self=False):
            op(eng, lambda e: e.scalar_tensor_tensor(out=out, in0=in0, scalar=scalar, in1=in1, op0=op0, op1=op1),
               reads=reads, writes=writes, skip_self=skip_self)

        def vcopy(out, in_, reads, writes, eng="vector"):
            op(eng, lambda e: e.tensor_copy(out=out, in_=in_), reads=reads, writes=writes)

        def rstd_from_ss(dst, ss_ap, n, reads, writes):
            act(dst, ss_ap, AF.Sqrt, reads, writes, scale=1.0 / n, bias=epsc[:])
            op("vector", lambda e: e.reciprocal(out=dst, in_=dst), reads=writes, writes=writes)

        def adaln_gen(l, slots=None):
            mod, modA, adab = mods[l], modAs[l], adabs[l]
            fw.dma("sync", adab[:], I.adab[l], l_c, writes=[t_mod])
            pb, tp = pbank[7], t_pb[7]
            pv = pb[:, 0:192].rearrange("p (a b) -> p a b", b=2)
            specs = []
            for s_ in range(48):
                specs.append((I.adaw[l, :, s_ * 256:(s_ + 1) * 256].rearrange("(c p) n -> p c n", p=128), 16, 256))
            ws = WStream(specs, slots)

            def finish(lo, hi):
                vtt(mod[:, lo:hi, :], pv[:, lo:hi, :], adab[:, lo:hi].unsqueeze(2).to_broadcast([128, hi - lo, 2]), ALU.add,
                    [tp, t_mod], [t_mod])

            def mk_modA(n):
                sc0, gi = ((16, 2 * l), (64, 2 * l + 1))[n]
                for r in range(2):
                    vstt(modA[:, n, r, :], mod[:, sc0:sc0 + 16, r], 1.0, gfm[:, gi, :], ALU.add, ALU.mult,
                         [t_mod, t_const], [t_mod])

            for s_ in range(48):
                w, tw = ws.get()
                for j in range(2):
                    oc = s_ * 2 + j
                    for c in range(16):
                        mm(pv[:, oc, :], w[:, c, j * 128:(j + 1) * 128], condb[:, c, :], c == 0, c == 15,
                           [tw, t_const], [tp], signal=(c == 15 and j == 1))
                ws.done()
                if s_ == 15:
                    finish(0, 32)
                    mk_modA(0)
                yield
            finish(32, 96)
            mk_modA(1)
            if debug and l == 0:
                fw.dma("sync", dbg["mod"], mod[:], l_c, reads=[t_mod])

        def adaln(l):
            for _ in adaln_gen(l):
                pass

        def shift(n, r):
            base = 0 if n == 0 else 48
            mod = mods[CUR[0]]
            return lambda c: mod[:, base + c, r:r + 1]

        def gate(n, r):
            base = 32 if n == 0 else 80
            mod = mods[CUR[0]]
            return lambda c: mod[:, base + c, r:r + 1]

        def modA_col(n, r):
            modA = modAs[CUR[0]]
            return lambda c: modA[:, n, r, c:c + 1]

        NSC = {}
        nsc_n = [0]
        nrm_k = [0]

        def alloc_norm_scratch(stk, with_xst=True, nbuf=1):
            k = nsc_n[0]
            nsc_n[0] += 1
            NSC["nbuf"] = nbuf
            NSC["sq"] = [fw.sbuf(f"nrm_sq{k}_{i}", [128, 16, 128], BF16, stk) for i in range(nbuf)]
            NSC["x"] = [fw.sbuf(f"nrm_x{k}_{i}", [128, 16, 128], F32, stk) for i in range(nbuf)]
            NSC["r"] = [fw.sbuf(f"nrm_r{k}_{i}", [128, 128], F32, stk) for i in range(nbuf)]
            NSC["t"] = [([T(f"nsq{q}") for q in range(4)], [T(f"nx{q}") for q in range(4)], T("nr")) for i in range(nbuf)]
            if with_xst:
                NSC["xst"] = [fw.sbuf(f"xst{k}_{i}", [128, D], F32, stk) for i in range(2)]

        def norm_tile(src, t_src, dst_fn, t_dst, Acol, Scol, src_is_psum_banks=None):
            bi = nrm_k[0] % NSC["nbuf"]
            nrm_k[0] += 1
            nsq, nx, nr = NSC["sq"][bi], NSC["x"][bi], NSC["r"][bi]
            tq_sq, tq_x, t_nr = NSC["t"][bi]
            if src_is_psum_banks is not None:
                for q, (pb, tp) in enumerate(src_is_psum_banks):
                    pv = pb[:].rearrange("p (a b) -> p a b", b=128)
                    vcopy(nx[:, 4 * q:4 * q + 4, :], pv, [tp], [tq_x[q]])
                    act(nsq[:, 4 * q:4 * q + 4, :], nx[:, 4 * q:4 * q + 4, :], AF.Square, [tq_x[q]], [tq_sq[q]])
                xs_ap, t_xs = nx[:], tq_x
            else:
                for q in range(4):
                    act(nsq[:, 4 * q:4 * q + 4, :], src[:, 4 * q:4 * q + 4, :], AF.Square, [t_src], [tq_sq[q]])
                xs_ap, t_xs = src, [t_src]
            pb, tp = pget()
            for c in range(16):
                mm(pb[:, 0:128], onesb[:], nsq[:, c, :], c == 0, c == 15, [tq_sq[c // 4], t_const], [tp])
            rstd_from_ss(nr[:], pb[:, 0:128], float(D), [tp, t_const], [t_nr])
            vtt(nx[:], xs_ap, nr[:].unsqueeze(1).to_broadcast([128, 16, 128]), ALU.mult,
                t_xs + [t_nr], tq_x)
            for c in range(16):
                vts(dst_fn(c), nx[:, c, :], Acol(c), Scol(c), ALU.mult, ALU.add, [tq_x[c // 4], t_mod, t_const], [t_dst])

        t_xst = [T(f"xst{i}") for i in range(2)]
        l_xst = [fw.lane(f"xst{i}") for i in range(2)]

        def load_tok_tile(k, tt):
            s = k % 2
            src = I.x[tt * 128:(tt + 1) * 128, :] if tt < 16 else I.ctx[(tt - 16) * 128:(tt - 15) * 128, :]
            fw.dma("sync", NSC["xst"][s][:], src, l_xst[s], writes=[t_xst[s]])

        def transpose_tile_to_banks(k):
            s = k % 2
            banks = []
            for q in range(4):
                pb, tp = pget()
                for j in range(4):
                    c = 4 * q + j
                    tr(pb[:, j * 128:(j + 1) * 128], NSC["xst"][s][:, c * 128:(c + 1) * 128], ident,
                       [t_xst[s], t_const], [tp], signal=(j == 3))
                banks.append((pb, tp))
            return banks

        ada0 = adaln_gen(0)
        for _ in range(16):
            next(ada0)
        if stop_after == "adaln":
            for _ in ada0:
                pass
        if stop_after == "adaln":
            fw.emit()
            return nc

        st0 = ExitStack()
        hT = fw.sbuf("hT", [128, 16, 2304], BF16, st0)
        t_h = T("hT")
        stn = ExitStack()
        alloc_norm_scratch(stn, nbuf=2)

        tiles = list(range(18))
        if stop_after == "n1tile":
            tiles = [0]
        set_pools(5)
        pool1_keep = pool_n0[0]
        load_tok_tile(0, tiles[0])
        for k, tt in enumerate(tiles):
            if k + 1 < len(tiles):
                load_tok_tile(k + 1, tiles[k + 1])
            for _ in range(2):
                next(ada0, None)
            banks = transpose_tile_to_banks(k)
            r = 0 if tt < 16 else 1
            norm_tile(None, None, lambda c, tt=tt: hT[:, c, tt * 128:(tt + 1) * 128], t_h,
                      modA_col(0, r), shift(0, r), src_is_psum_banks=banks)
        for _ in ada0:
            pass
        set_pools(6)
        fw.barrier()
        if debug and stop_after != "norm1x":
            fw.dma("sync", dbg["hT"], hT[:], l_c, reads=[t_h])

        if stop_after == "n1tile":
            fw.dma("sync", dbg["x1"][:, :, 0:128], NSC["x"][0][:], l_c, reads=NSC["t"][0][1])
        if stop_after in ("norm1", "norm1x", "n1tile"):
            fw.barrier()
            fw.emit()
            st0.close()
            return nc

        stn.close()
        def h_own(c, o0, n):
            if o0 < NOWN:
                return hT[:, c, o0:o0 + n]
            return hT[:, c, 2048 + (o0 - NOWN):2048 + (o0 - NOWN) + n]
        OWN_BLOCKS = [(0, 512), (512, 512), (1024, 256), (1280, 256)]
        ALL_BLOCKS = [(0, 512), (512, 512), (1024, 512), (1536, 512), (2048, 256)]

        OWN_TILES = [(ot, ot) for ot in range(10)] + [(10, 16), (11, 17)]
        mixst = fw.sbuf("mixst", [128, 2, 1536], BF16, st0)
        t_mixst = T("mixst")
        l_mix = fw.lane("mix")
        t_scr = T("mix_scr")

        def fm_proj(w, tw, blocks, evac, src=None, t_src=None):
            src = hT if src is None else src
            t_src = t_h if t_src is None else t_src
            prev = None
            for (t0, n) in blocks:
                pb, tp = pget()
                for c in range(16):
                    mm(pb[:, 0:n], w[:, c, :], src[:, c, t0:t0 + n], c == 0, c == 15, [tw, t_src], [tp])
                if prev is not None:
                    evac(*prev)
                prev = (pb, tp, t0, n)
            evac(*prev)

        def ssd_phase():
            ss = ExitStack()
            rawbuf = fw.sbuf("rawbuf", [128, 2312], F32, ss)
            cacc = fw.sbuf("cacc", [128, 2304], F32, ss)
            xTb = fw.sbuf("xTb", [128, 2304], BF16, ss)
            bT = fw.sbuf("bT", [128, 2304], BF16, ss)
            cT = fw.sbuf("cT", [128, 2304], BF16, ss)
            xs_tm = fw.sbuf("xs_tm", [128, 18, 256], BF16, ss)
            b_tm = fw.sbuf("b_tm", [128, 18, 128], BF16, ss)
            zs = fw.sbuf("zs", [128, 12, 256], BF16, ss)
            dtall = fw.sbuf("dtall", [128, 18, 32], F32, ss)
            aall = fw.sbuf("aall", [128, 18, 32], F32, ss)
            ea = fw.sbuf("ea", [128, 18, 32], F32, ss)
            dtd = fw.sbuf("dtd", [128, 18, 32], F32, ss)
            cd = fw.sbuf("cd", [128, 18, 32], F32, ss)
            hstb = fw.sbuf("hstb", [128, 12, 256], BF16, ss)
            S = fw.sbuf("S", [128, 256], F32, ss)
            Sb = fw.sbuf("Sb", [128, 256], BF16, ss)
            stmp = fw.sbuf("stmp", [128, 256], F32, ss)
            vecb = fw.sbuf("vecb", [128, 80], F32, ss)
            Abc = fw.sbuf("Abc", [128, 32], F32, ss)
            gnb = fw.sbuf("gnb", [128, 256], F32, ss)
            t_gn = T("gnb")
            cw = fw.sbuf("cw", [128, 16, 5], F32, ss)
            cb = fw.sbuf("cb", [128, 16], F32, ss)
            sm = fw.sbuf("sm", [128, 2, 64], F32, ss)
            xdt = [fw.sbuf(f"xdt{i}", [128, 256], BF16, ss) for i in range(2)]
            xdtd = fw.sbuf("xdtd", [128, 256], BF16, ss)
            CBm = [fw.sbuf(f"CBm{i}", [128, 128], F32, ss) for i in range(2)]
            aU4 = [fw.sbuf(f"aU4_{i}", [128, 4, 128], F32, ss) for i in range(2)]
            Mt4 = [xTb[:, 1024 * i:1024 * (i + 1)].bitcast(F32) for i in range(2)]
            Mtb4 = [fw.sbuf(f"Mtb4_{i}", [128, 4, 128], BF16, ss) for i in range(2)]
            t_aU4 = [T("aU4_0"), T("aU4_1")]
            t_Mt4 = [T("Mt4_0"), T("Mt4_1")]
            t_Mtb4 = [T("Mtb4_0"), T("Mtb4_1")]
            yo2 = [fw.sbuf(f"yo{i}", [128, 256], F32, ss) for i in range(2)]
            yy2 = [fw.sbuf(f"yy{i}", [128, 256], F32, ss) for i in range(2)]
            ysq2 = [fw.sbuf(f"ysq{i}", [128, 256], F32, ss) for i in range(2)]
            ynb2 = [fw.sbuf(f"ynb{i}", [128, 256], BF16, ss) for i in range(2)]
            ssq2 = [fw.sbuf(f"ssq{i}", [128, 2], F32, ss) for i in range(2)]
            t_raw, t_cacc, t_xTb, t_bT, t_cT, t_xs, t_btm, t_zs = (T(n) for n in "raw cacc xTb bT cT xs btm zs".split())
            t_dt, t_hst, t_S, t_Sb, t_stmp, t_vec, t_sm = (T(n) for n in "dt hst S Sb stmp vec sm".split())
            t_xdt = [T("xdt0"), T("xdt1")]
            t_xdtd = T("xdtd")
            t_CBm = [T("cbm0"), T("cbm1")]
            t_aU = [T("aU0"), T("aU1")]
            t_Mt = [T("Mt0"), T("Mt1")]
            t_Mtb = [T("Mtb0"), T("Mtb1")]
            t_y2 = [[T(n + str(i)) for n in "yo yy ysq ynb ssq".split()] for i in range(2)]
            ypar = [0]
            if os.environ.get("SBUF_REPORT"):
                print("SBUF remaining in SSD phase:", nc.sbuf_bytes_remaining)

            fw.dma("sync", vecb[:].unsqueeze(1), I.vec[0:1, :].partition_broadcast(128), l_c, writes=[t_vec])
            fw.dma("sync", cw[:], I.convw, l_c, writes=[t_vec])
            fw.dma("sync", cb[:], I.convb, l_c, writes=[t_vec])
            dtb_bc, alog_bc, dsk_bc = vecb[:, 0:32], vecb[:, 32:64], vecb[:, 64:80]
            act(Abc[:], alog_bc, AF.Exp, [t_vec], [t_vec])
            vts(Abc[:], Abc[:], -1.0, None, ALU.mult, None, [t_vec], [t_vec])
            op("vector", lambda e: e.memset(rawbuf[:], 0.0), writes=[t_raw])

            specs = [(I.win0[:, 3072:3104].rearrange("(c p) n -> p c n", p=128), 16, 32)]
            for g in range(4):
                for col0 in (1024 + g * 256, 1024 + g * 256 + 128, 2048 + g * 128, 2560 + g * 128):
                    specs.append((I.win0[:, col0:col0 + 128].rearrange("(c p) n -> p c n", p=128), 16, 128))
                specs.append((I.win0[:, g * 256:(g + 1) * 256].rearrange("(c p) n -> p c n", p=128), 16, 256))
            ws = WStream(specs)

            w, tw = ws.get()
            for tt in range(18):
                pb, tp = pget()
                for c in range(16):
                    mm(pb[:, 0:32], hT[:, c, tt * 128:(tt + 1) * 128], w[:, c, :], c == 0, c == 15, [tw, t_h], [tp])
                vtt(dtall[:, tt, :], pb[:, 0:32], dtb_bc, ALU.add, [tp, t_vec], [t_dt])
                act(dtall[:, tt, :], dtall[:, tt, :], AF.Exp, [t_dt], [t_dt])
                act(dtall[:, tt, :], dtall[:, tt, :], AF.Ln, [t_dt, t_const], [t_dt], bias=onec[:])
                vtt(aall[:, tt, :], dtall[:, tt, :], Abc[:], ALU.mult, [t_dt, t_vec], [t_dt])
                pb2, tp2 = pget()
                mm(pb2[:, 0:16], LE, aall[:, tt, 0:16], True, True, [t_dt, t_const], [tp2], signal=False)
                mm(pb2[:, 16:32], GE, aall[:, tt, 16:32], True, True, [t_dt, t_const], [tp2], signal=False)
                mm(pb2[:, 32:64], onesf[:], aall[:, tt, :], True, True, [t_dt, t_const], [tp2], signal=True)
                vcopy(sm[:, 0, 0:32], pb2[:, 0:32], [tp2], [t_sm])
                vtt(sm[:, 1, 0:32], pb2[:, 32:64], sm[:, 0, 0:32], ALU.subtract, [tp2, t_sm], [t_sm])
                act(cd[:, tt, :], pb2[:, 32:64], AF.Exp, [tp2], [t_dt])
                act(ea[:, tt, :], sm[:, 0, 0:32], AF.Exp, [t_sm], [t_dt])
                act(dtd[:, tt, :], sm[:, 1, 0:32], AF.Exp, [t_sm], [t_dt])
                vtt(dtd[:, tt, :], dtd[:, tt, :], dtall[:, tt, :], ALU.mult, [t_dt], [t_dt])
            ws.done()

            def conv_chunk(ch, dst, t_dst):
                for (o0, n, r0) in ((0, 2048, 0), (2048, 256, 2052)):
                    vts(cacc[:, o0:o0 + n], rawbuf[:, r0:r0 + n], cw[:, ch, 0:1], None, ALU.mult, None,
                        [t_raw, t_vec], [t_cacc])
                    for k in range(1, 5):
                        vstt(cacc[:, o0:o0 + n], rawbuf[:, r0 + k:r0 + k + n], cw[:, ch, k:k + 1], cacc[:, o0:o0 + n],
                             ALU.mult, ALU.add, [t_raw, t_vec, t_cacc], [t_cacc])
                act(dst[:], cacc[:], AF.Silu, [t_cacc, t_vec], (t_dst if isinstance(t_dst, list) else [t_dst]),
                    bias=cb[:, ch:ch + 1])

            def raw_evac(pb, tp, t0, n):
                off = 2 + t0 if t0 < 2048 else 2054 + (t0 - 2048)
                act(rawbuf[:, off:off + n], pb[:, 0:n], AF.Copy, [tp], [t_raw])

            def to_tm(src, t_src, dst3, t_dst, j):
                for q in range(5):
                    tts = list(range(4 * q, min(4 * q + 4, 18)))
                    pb, tp = pget()
                    for i, tt in enumerate(tts):
                        mm(pb[:, i * 128:(i + 1) * 128], src[:, tt * 128:(tt + 1) * 128], identb[:], True, True,
                           (t_src if isinstance(t_src, list) else [t_src]) + [t_const], [tp], signal=(i == len(tts) - 1))
                    nn = len(tts)
                    act(dst3[:, tts[0]:tts[0] + nn, j * 128:(j + 1) * 128],
                        pb[:, 0:nn * 128].rearrange("p (a b) -> p a b", b=128), AF.Copy, [tp], [t_dst])

            for g in range(4):
                fw.dma("sync", gnb[:].unsqueeze(1), I.ssmg[0:1, g * 256:(g + 1) * 256].partition_broadcast(128), l_c,
                       writes=[t_gn])
                for j in range(2):
                    w, tw = ws.get()
                    fm_proj(w, tw, ALL_BLOCKS, raw_evac)
                    ws.done()
                    conv_chunk(2 * g + j, xTb, [t_xTb] + t_Mt4)
                    to_tm(xTb, [t_xTb] + t_Mt4, xs_tm, t_xs, j)
                w, tw = ws.get()
                fm_proj(w, tw, ALL_BLOCKS, raw_evac)
                ws.done()
                conv_chunk(8 + g, bT, t_bT)
                to_tm(bT, t_bT, b_tm, t_btm, 0)
                w, tw = ws.get()
                fm_proj(w, tw, ALL_BLOCKS, raw_evac)
                ws.done()
                conv_chunk(12 + g, cT, t_cT)
                w, tw = ws.get()
                for ot, tt in OWN_TILES:
                    pb, tp = pget()
                    for c in range(16):
                        mm(pb[:, 0:256], hT[:, c, tt * 128:(tt + 1) * 128], w[:, c, :], c == 0, c == 15, [tw, t_h], [tp])
                    act(zs[:, ot, :], pb[:, 0:256], AF.Silu, [tp], [t_zs])
                ws.done()

                own_of = {tt: ot for ot, tt in OWN_TILES}

                def state_update(d, tt):
                    hc = slice(d * 16 + 4 * g, d * 16 + 4 * g + 4)
                    vtt(xdtd[:].rearrange("p (h e) -> p h e", e=64), xs_tm[:, tt, :].rearrange("p (h e) -> p h e", e=64),
                        dtd[:, tt, hc].unsqueeze(2).to_broadcast([128, 4, 64]), ALU.mult, [t_xs, t_dt], [t_xdtd])
                    pb, tp = pget()
                    mm(pb[:, 0:256], b_tm[:, tt, :], xdtd[:], True, True, [t_btm, t_xdtd], [tp])
                    vtt(stmp[:].rearrange("p (h e) -> p h e", e=64), S[:].rearrange("p (h e) -> p h e", e=64),
                        cd[:, tt, hc].unsqueeze(2).to_broadcast([128, 4, 64]), ALU.mult, [t_S, t_dt], [t_stmp])
                    vtt(S[:], stmp[:], pb[:, 0:256], ALU.add, [t_stmp, tp], [t_S])
                    act(Sb[:], S[:], AF.Copy, [t_S], [t_Sb])

                def y_tile(ot, tt):
                    tok = slice(tt * 128, (tt + 1) * 128)
                    yb_ = ypar[0] % 2
                    ypar[0] += 1
                    yo, yy, ysq, ynb, ssq = yo2[yb_], yy2[yb_], ysq2[yb_], ynb2[yb_], ssq2[yb_]
                    t_yo, t_yy, t_ysq, t_ynb, t_ssq = t_y2[yb_]
                    pcb, tpcb = pget()
                    mm(pcb[:, 0:128], bT[:, tok], cT[:, tok], True, True, [t_bT, t_cT], [tpcb])
                    vtt(CBm[0][:], pcb[:, 0:128], LE, ALU.mult, [tpcb, t_const], [t_CBm[0]])
                    vtt(CBm[1][:], pcb[:, 0:128], GE, ALU.mult, [tpcb, t_const], [t_CBm[1]])
                    for d in range(2):
                        hc = slice(d * 16 + 4 * g, d * 16 + 4 * g + 4)
                        vtt(xdt[d][:].rearrange("p (h e) -> p h e", e=64), xs_tm[:, tt, :].rearrange("p (h e) -> p h e", e=64),
                            dtall[:, tt, hc].unsqueeze(2).to_broadcast([128, 4, 64]), ALU.mult, [t_xs, t_dt], [t_xdt[d]])
                        vtt(aU4[d][:], (GT if d == 0 else LT).unsqueeze(1).to_broadcast([128, 4, 128]),
                            aall[:, tt, hc].unsqueeze(2).to_broadcast([128, 4, 128]), ALU.mult, [t_const, t_dt], [t_aU4[d]])
                        psg, tpsg = pget()
                        for hh in range(4):
                            mm(psg[:, hh * 128:(hh + 1) * 128], aU4[d][:, hh, :], (LE if d == 0 else GE), True, True,
                               [t_aU4[d], t_const], [tpsg], signal=(hh == 3))
                        act(Mt4[d], psg[:], AF.Exp, [tpsg], [t_Mt4[d]])
                        vtt(Mtb4[d][:], Mt4[d].rearrange("p (h s) -> p h s", s=128),
                            CBm[d][:].unsqueeze(1).to_broadcast([128, 4, 128]), ALU.mult, [t_Mt4[d], t_CBm[d]], [t_Mtb4[d]])
                    py, tpy = pget(1)
                    for hh in range(4):
                        for d in range(2):
                            mm(py[:, hh * 64:(hh + 1) * 64], Mtb4[d][:, hh, :], xdt[d][:, hh * 64:(hh + 1) * 64], d == 0, d == 1,
                               [t_Mtb4[d], t_xdt[d]], [tpy], signal=(d == 1 and hh == 3))
                    pyo, tpyo = pget(1)
                    mm(pyo[:, 0:256], cT[:, tok], Sb[:], True, True, [t_cT, t_Sb], [tpyo], signal=False)
                    mm(pyo[:, 256:512], cT[:, tok], hstb[:, ot, :], True, True, [t_cT, t_hst], [tpyo], signal=True)
                    for d in range(2):
                        hc = slice(d * 16 + 4 * g, d * 16 + 4 * g + 4)
                        dst = yo if d == 0 else ysq
                        vtt(dst[:].rearrange("p (h e) -> p h e", e=64),
                            pyo[:, d * 256:(d + 1) * 256].rearrange("p (h e) -> p h e", e=64),
                            ea[:, tt, hc].unsqueeze(2).to_broadcast([128, 4, 64]), ALU.mult, [tpyo, t_dt],
                            [t_yo if d == 0 else t_ysq])
                    vtt(yy[:], py[:, 0:256], yo[:], ALU.add, [tpy, t_yo], [t_yy])
                    vtt(yy[:], yy[:], ysq[:], ALU.add, [t_yy, t_ysq], [t_yy])
                    vtt(yo[:].rearrange("p (h e) -> p h e", e=64), xs_tm[:, tt, :].rearrange("p (h e) -> p h e", e=64),
                        dsk_bc[:, 4 * g:4 * g + 4].unsqueeze(2).to_broadcast([128, 4, 64]), ALU.mult, [t_xs, t_vec], [t_yo])
                    vtt(yy[:], yy[:], yo[:], ALU.add, [t_yy, t_yo], [t_yy])
                    vtt(yy[:], yy[:], zs[:, ot, :], ALU.mult, [t_yy, t_zs], [t_yy])
                    act(ysq[:], yy[:], AF.Square, [t_yy], [t_ysq, t_ssq], accum_out=ssq[:, 0:1])
                    rstd_from_ss(ssq[:, 1:2], ssq[:, 0:1], 256.0, [t_ssq, t_const], [t_ssq])
                    vstt(ynb[:], yy[:], ssq[:, 1:2], gnb[:], ALU.mult, ALU.mult,
                         [t_yy, t_ssq, t_gn], [t_ynb])
                    ptr, tptr = pget()
                    for j in range(2):
                        mm(ptr[:, j * 128:(j + 1) * 128], ynb[:, j * 128:(j + 1) * 128], identb[:], True, True,
                           [t_ynb, t_const], [tptr], signal=(j == 1))
                    vcopy(mixst[:, :, ot * 128:(ot + 1) * 128], ptr[:, 0:256].rearrange("p (a b) -> p a b", b=128),
                          [tptr], [t_mixst])

                op("vector", lambda e: e.memset(S[:], 0.0), writes=[t_S])
                op("vector", lambda e: e.memset(Sb[:], 0.0), writes=[t_Sb])
                order = [17, 16] + list(range(15, -1, -1))
                for idx, tt in enumerate(order):
                    if tt in own_of:
                        vcopy(hstb[:, own_of[tt], :], Sb[:], [t_Sb], [t_hst])
                    if idx < len(order) - 1:
                        state_update(1, tt)
                op("vector", lambda e: e.memset(S[:], 0.0), writes=[t_S])
                op("vector", lambda e: e.memset(Sb[:], 0.0), writes=[t_Sb])
                order = [16, 17] + list(range(10))
                for idx, tt in enumerate(order):
                    y_tile(own_of[tt], tt)
                    if idx < len(order) - 1:
                        state_update(0, tt)
                fw.dma("sync", mix_scr[2 * g:2 * g + 2].rearrange("a p t -> p a t"), mixst[:], l_mix,
                       reads=[t_mixst], writes=[t_scr])
            fw.barrier()
            ss.close()

        if os.environ.get('SKIP_SSD') != '1':
            ssd_phase()
        if debug and stop_after == "ssd":
            fw.dma("sync", dbg["mix"][0:8], mix_scr[0:8], l_c, reads=[t_scr])
            fw.barrier()
            fw.emit()
            st0.close()
            return nc

        def diff_phase():
            ds = ExitStack()
            cosT = fw.sbuf("cosT", [128, NLAT], F32, ds)
            sinS = fw.sbuf("sinS", [128, NLAT], F32, ds)
            qT = fw.sbuf("qT", [128, 1536], BF16, ds)
            qTm = [fw.sbuf(f"qTm{i}", [128, 1536], BF16, ds) for i in range(2)]
            t_qTm = [T("qTm0"), T("qTm1")]
            kT = fw.sbuf("kT", [128, 2304], BF16, ds)
            v_tm = fw.sbuf("v_tm", [128, 18, 128], BF16, ds)
            qraw = fw.sbuf("qraw", [128, 512], F32, ds)
            r1 = fw.sbuf("r1", [128, 512], F32, ds)
            r2 = fw.sbuf("r2", [128, 512], F32, ds)
            pT = [fw.sbuf(f"pT{i}", [128, 512], BF16, ds) for i in range(6)]
            oe = [fw.sbuf(f"oe{i}", [128, 512], F32, ds) for i in range(2)]
            rzb = [fw.sbuf(f"rzb{i}", [128, 512], F32, ds) for i in range(2)]
            t_rzb = [T("rzb0"), T("rzb1")]
            rz = fw.sbuf("rz", [128, 512], F32, ds)
            osq = fw.sbuf("osq", [128, 512], BF16, ds)
            lamb = fw.sbuf("lamb", [128, 256], F32, ds)
            lams = fw.sbuf("lams", [128, 4], F32, ds)
            sg = fw.sbuf("sg", [128, 1], F32, ds)
            t_rope, t_qT, t_kT, t_v, t_qraw, t_r1, t_r2, t_rz, t_osq, t_lam = (T(n) for n in
                                                                            "rope qT kT v qraw r1 r2 rz osq lam".split())
            t_pT = [T(f"pT{i}") for i in range(6)]
            t_oe = [T("oe0"), T("oe1")]
            fw.dma("sync", cosT[:], I.cos, l_c, writes=[t_rope])
            fw.dma("sync", sinS[:], I.sin, l_c, writes=[t_rope])
            fw.dma("sync", lamb[:].unsqueeze(1), I.lamv[0:1, :].partition_broadcast(128), l_c, writes=[t_lam])
            fw.dma("sync", sg[:], I.subg, l_c, writes=[t_lam])
            vtt(lamb[:, 0:64], lamb[:, 0:64], lamb[:, 64:128], ALU.mult, [t_lam], [t_lam])
            vtt(lamb[:, 128:192], lamb[:, 128:192], lamb[:, 192:256], ALU.mult, [t_lam], [t_lam])
            op("vector", lambda e: e.reduce_sum(out=lams[:, 0:1], in_=lamb[:, 0:64], axis=AX.X), reads=[t_lam], writes=[t_lam])
            op("vector", lambda e: e.reduce_sum(out=lams[:, 1:2], in_=lamb[:, 128:192], axis=AX.X), reads=[t_lam], writes=[t_lam])
            act(lams[:, 0:2], lams[:, 0:2], AF.Exp, [t_lam], [t_lam])
            vtt(lams[:, 2:3], lams[:, 1:2], lams[:, 0:1], ALU.subtract, [t_lam], [t_lam])
            vts(lams[:, 3:4], lams[:, 2:3], -LAMBDA_INIT0, None, ALU.add, None, [t_lam], [t_lam])
            vts(sg[:], sg[:], 1.0 - LAMBDA_INIT0, None, ALU.mult, None, [t_lam], [t_lam])

            if os.environ.get("SBUF_REPORT"):
                print("SBUF remaining in diff phase:", nc.sbuf_bytes_remaining)
            for e in range(2):
                op("vector", lambda en, e=e: en.memset(qTm[e][:], 0.0), writes=[t_qTm[e]])
            specs = []
            for hc in range(8):
                for col0 in (3104 + hc * 128, 4128 + hc * 128, 5152 + hc * 128):
                    specs.append((I.win0[:, col0:col0 + 128].rearrange("(c p) n -> p c n", p=128), 16, 128))
            ws = WStream(specs)

            def rope_evac(dst, t_dst, own_map):
                def ev(pb, tp, t0, n):
                    d0 = own_map(t0)
                    if t0 >= 2048:
                        act(dst[:, d0:d0 + n], pb[:, 0:n], AF.Copy, [tp], [t_dst])
                        return
                    act(qraw[:, 0:n], pb[:, 0:n], AF.Copy, [tp], [t_qraw])
                    pp, tpp = pget()
                    mm(pp[:, 0:n], PM, qraw[:, 0:n], True, True, [t_const, t_qraw], [tpp])
                    vtt(r1[:, 0:n], qraw[:, 0:n], cosT[:, t0:t0 + n], ALU.mult, [t_qraw, t_rope], [t_r1])
                    vtt(r2[:, 0:n], pp[:, 0:n], sinS[:, t0:t0 + n], ALU.mult, [tpp, t_rope], [t_r2])
                    vtt(dst[:, d0:d0 + n], r1[:, 0:n], r2[:, 0:n], ALU.add, [t_r1, t_r2], [t_dst])
                return ev

            QBLOCKS = [(0, 512), (512, 512), (1024, 256), (2048, 256)]
            own_idx = lambda t0: t0 if t0 < 2048 else NOWN + (t0 - 2048)
            pk = 0
            set_pools(3, 7)
            asl = [fw.sbuf(f"asl{i}", [128, 4096], BF16, ds) for i in range(2)]
            ada1 = adaln_gen(1, (asl, [T("asl0"), T("asl1")], [fw.lane("asl0"), fw.lane("asl1")]))
            for hc in range(8):
                w, tw = ws.get()
                fm_proj(w, tw, QBLOCKS, rope_evac(qT, t_qT, own_idx))
                ws.done()
                w, tw = ws.get()
                fm_proj(w, tw, ALL_BLOCKS, rope_evac(kT, t_kT, lambda t0: t0))
                ws.done()
                w, tw = ws.get()
                for q4 in range(5):
                    tts = list(range(4 * q4, min(4 * q4 + 4, 18)))
                    pb, tp = pget()
                    for i, tt in enumerate(tts):
                        for c in range(16):
                            mm(pb[:, i * 128:(i + 1) * 128], hT[:, c, tt * 128:(tt + 1) * 128], w[:, c, :], c == 0, c == 15,
                               [tw, t_h], [tp], signal=(c == 15 and i == len(tts) - 1))
                    nn = len(tts)
                    act(v_tm[:, tts[0]:tts[0] + nn, :], pb[:, 0:nn * 128].rearrange("p (a b) -> p a b", b=128), AF.Copy,
                        [tp], [t_v])
                ws.done()
                for (q0, n) in QBLOCKS:
                    o0 = own_idx(q0)
                    keyt = list(range(18)) if q0 < 2048 else [16, 17]
                    for e in range(2):
                        pr = slice(64 * e, 64 * e + 64)
                        vcopy(qTm[e][pr, o0:o0 + n], qT[pr, o0:o0 + n], [t_qT], [t_qTm[e]])
                        PO, tPO = pget(1)
                        PZ, tPZ = pget(1)
                        next(ada1, None)
                        LOOK = 2
                        slots = []
                        for ki in range(len(keyt) + LOOK):
                            if ki < len(keyt):
                                kt = keyt[ki]
                                ps, tps = pget()
                                mm(ps[:, 0:n], kT[:, kt * 128:(kt + 1) * 128], qTm[e][:, o0:o0 + n], True, True,
                                   [t_kT, t_qTm[e]], [tps])
                                s = pk % 6
                                pk += 1
                                act(pT[s][:, 0:n], ps[:, 0:n], AF.Exp, [tps], [t_pT[s]], scale=0.125)
                                slots.append(s)
                            kj = ki - LOOK
                            if kj >= 0:
                                kt = keyt[kj]
                                s = slots[kj]
                                first, last = kj == 0, kj == len(keyt) - 1
                                mm(PO[:, 0:n], v_tm[:, kt, :], pT[s][:, 0:n], first, last, [t_v, t_pT[s]], [tPO])
                                mm(PZ[:, 0:n], onesb[:], pT[s][:, 0:n], first, last, [t_const, t_pT[s]], [tPZ])
                        op("vector", lambda en, n=n, PZ=PZ, e=e: en.reciprocal(out=rzb[e][:, 0:n], in_=PZ[:, 0:n]),
                           reads=[tPZ], writes=[t_rzb[e]])
                        vtt(oe[e][:, 0:n], PO[:, 0:n], rzb[e][:, 0:n], ALU.mult, [tPO, t_rzb[e]], [t_oe[e]])
                    vstt(oe[0][:, 0:n], oe[1][:, 0:n], lams[:, 3:4], oe[0][:, 0:n], ALU.mult, ALU.add,
                         [t_oe[0], t_oe[1], t_lam], [t_oe[0]])
                    act(osq[:, 0:n], oe[0][:, 0:n], AF.Square, [t_oe[0]], [t_osq])
                    pss, tpss = pget()
                    mm(pss[:, 0:n], onesb[:], osq[:, 0:n], True, True, [t_const, t_osq], [tpss])
                    rstd_from_ss(rz[:, 0:n], pss[:, 0:n], 128.0, [tpss, t_const], [t_rz])
                    vtt(oe[0][:, 0:n], oe[0][:, 0:n], rz[:, 0:n], ALU.mult, [t_oe[0], t_rz], [t_oe[0]])
                    act(mixst[:, hc % 2, o0:o0 + n], oe[0][:, 0:n], AF.Identity, [t_oe[0], t_lam], [t_mixst], scale=sg[:, 0:1])
                if hc % 2 == 1:
                    fw.dma("sync", mix_scr[8 + hc - 1:8 + hc + 1].rearrange("a p t -> p a t"), mixst[:], l_mix,
                           reads=[t_mixst], writes=[t_scr])
            for _ in ada1:
                pass
            fw.barrier()
            set_pools(6)
            ds.close()

        if os.environ.get('SKIP_DIFF') != '1':
            diff_phase()
        if debug and stop_after == "diff":
            if os.environ.get('SKIP_SSD') == '1':
                fw.dma("sync", dbg["mix"][8:16], mix_scr[8:16], l_c, reads=[t_scr])
            else:
                fw.dma("sync", dbg["mix"], mix_scr, l_c, reads=[t_scr])
            fw.barrier()
            fw.emit()
            st0.close()
            return nc
        st0.close()

        xh["main"] = fw.sbuf("xh_main", [128, 16, 1024], F32)
        xh["b"] = fw.sbuf("xh_b", [128, 16, 512], F32)

        def xtile(ot):
            if ot < 8:
                return xh["main"][:, :, ot * 128:(ot + 1) * 128], t_x[ot // 4]
            return xh["b"][:, :, (ot - 8) * 128:(ot - 7) * 128], t_x[2 + (ot - 8) // 2]

        def dbg_x(name, nblk=4):
            if not debug:
                return
            for blk in range(nblk):
                xb, n = xblk(blk)
                fw.dma("sync", dbg[name][:, :, OWN0[blk]:OWN0[blk] + n], xb, l_c, reads=[t_x[blk]])

        def outproj0():
            ps_ = ExitStack()
            mixT = fw.sbuf("mixT", [128, 16, 1536], BF16, ps_)
            t_mixT = T("mixT")
            NSC["xst"] = [fw.sbuf(f"xstB_{i}", [128, D], F32, ps_) for i in range(2)]
            fw.dma("sync", mixT[:], mix_scr.rearrange("a p t -> p a t"), l_c, reads=[t_scr], writes=[t_mixT])
            load_tok_tile(0, OWN_TILES[0][1])
            for k, (ot, tt) in enumerate(OWN_TILES):
                if k + 1 < len(OWN_TILES):
                    load_tok_tile(k + 1, OWN_TILES[k + 1][1])
                banks = transpose_tile_to_banks(k)
                xt, tx = xtile(ot)
                for q, (pb, tp) in enumerate(banks):
                    vcopy(xt[:, 4 * q:4 * q + 4, :], pb[:].rearrange("p (a b) -> p a b", b=128), [tp], [tx])
            specs = [(I.wout0[:, dc * 128:(dc + 1) * 128].rearrange("(c p) n -> p c n", p=128), 16, 128) for dc in range(16)]
            ws = WStream(specs)
            for dc in range(16):
                w, tw = ws.get()
                for blk in range(4):
                    xb, n = xblk(blk)
                    r = 1 if blk == 3 else 0
                    pb, tp = pget()
                    for mc in range(16):
                        mm(pb[:, 0:n], w[:, mc, :], mixT[:, mc, OWN0[blk]:OWN0[blk] + n], mc == 0, mc == 15, [tw, t_mixT], [tp])
                    vstt(xb[:, dc, :], pb[:, 0:n], gate(0, r)(dc), xb[:, dc, :], ALU.mult, ALU.add,
                         [tp, t_mod, t_x[blk]], [t_x[blk]])
                ws.done()
            fw.barrier()
            ps_.close()

        outproj0()
        dbg_x("x1")
        if stop_after == "out0":
            fw.barrier()
            fw.emit()
            return nc

        def ffn(l, blocks):
            fs = ExitStack()
            ntok = sum(xblk(b)[1] for b in blocks)
            h2 = fw.sbuf(f"h2_{l}", [128, 16, ntok], BF16, fs)
            t_h2 = T("h2")
            alloc_norm_scratch(fs, with_xst=False, nbuf=(2 if l == 1 else 1))
            u = [fw.sbuf(f"u{l}_{i}", [128, 2, 512], BF16, fs) for i in range(2)]
            t_u = [T("u0"), T("u1")]
            sa = [fw.sbuf(f"sa{l}_{i}", [128, 512], F32, fs) for i in range(2)]
            t_sa = [T("sa0"), T("sa1")]
            for blk in blocks:
                xb, n = xblk(blk)
                r = 1 if blk == 3 else 0
                for j in range(n // 128):
                    o0 = OWN0[blk] + j * 128
                    norm_tile(xb[:, :, j * 128:(j + 1) * 128], t_x[blk], lambda c, o0=o0: h2[:, c, o0:o0 + 128], t_h2,
                              modA_col(1, r), shift(1, r))
            specs = []
            for fb in range(DFF // 256):
                specs.append((I.w1[l, :, fb * 256:(fb + 1) * 256].rearrange("(c p) n -> p c n", p=128), 16, 256))
                specs.append((I.w3[l, :, fb * 256:(fb + 1) * 256].rearrange("(c p) n -> p c n", p=128), 16, 256))
                specs.append((I.w2[l, fb * 256:(fb + 1) * 256, :].rearrange("(f p) n -> p f n", p=128), 2, 2048))
            ws = WStream(specs)
            ui = 0
            set_pools(4)
            pend = []

            def flush(k):
                for _ in range(min(k, len(pend))):
                    pend.pop(0)()

            NFB = DFF // 256
            for fb in range(NFB):
                w1s, tw1 = ws.slab(3 * fb)
                w3s, tw3 = ws.slab(3 * fb + 1)
                w2s, tw2 = ws.slab(3 * fb + 2)
                for bi_, blk in enumerate(blocks):
                    xb, n = xblk(blk)
                    r = 1 if blk == 3 else 0
                    o0 = OWN0[blk]
                    uu, tu = u[ui % 2], t_u[ui % 2]
                    ui += 1
                    for fc in range(2):
                        pa, tpa = pget(1)
                        for c in range(16):
                            mm(pa[:, 0:n], w1s[:, c, fc * 128:(fc + 1) * 128], h2[:, c, o0:o0 + n], c == 0, c == 15, [tw1, t_h2], [tpa])
                        flush(4)
                        pb3, tpb3 = pget(1)
                        for c in range(16):
                            mm(pb3[:, 0:n], w3s[:, c, fc * 128:(fc + 1) * 128], h2[:, c, o0:o0 + n], c == 0, c == 15, [tw3, t_h2], [tpb3])
                        flush(4)
                        act(sa[fc][:, 0:n], pa[:, 0:n], AF.Silu, [tpa], [t_sa[fc]])
                        vtt(uu[:, fc, 0:n], sa[fc][:, 0:n], pb3[:, 0:n], ALU.mult, [t_sa[fc], tpb3], [tu])
                    assert not pend
                    if bi_ == 0 and fb > 0:
                        ws.done()

                    def mk_down(dc, xb=xb, n=n, r=r, uu=uu, tu=tu, blk=blk, w2s=w2s, tw2=tw2):
                        def f():
                            po, tpo = pget()
                            for fc in range(2):
                                mm(po[:, 0:n], w2s[:, fc, dc * 128:(dc + 1) * 128], uu[:, fc, 0:n], fc == 0, fc == 1, [tw2, tu], [tpo])
                            vstt(xb[:, dc, :], po[:, 0:n], gate(1, r)(dc), xb[:, dc, :], ALU.mult, ALU.add,
                                 [tpo, t_mod, t_x[blk]], [t_x[blk]])
                        return f
                    pend.extend(mk_down(dc) for dc in range(16))
                ws.done()
                ws.done()
            flush(16)
            ws.done()
            fw.barrier()
            set_pools(6)
            fs.close()

        ffn(0, [0, 1, 2, 3])
        dbg_x("x2")
        if stop_after == "ffn0":
            fw.barrier()
            fw.emit()
            return nc

        CUR[0] = 1
        s1 = ExitStack()
        hT1 = fw.sbuf("hT1", [128, 16, 1536], BF16, s1)
        t_h1 = T("hT1")
        sn = ExitStack()
        alloc_norm_scratch(sn, with_xst=False)
        for blk in range(4):
            xb, n = xblk(blk)
            r = 1 if blk == 3 else 0
            for j in range(n // 128):
                o0 = OWN0[blk] + j * 128
                norm_tile(xb[:, :, j * 128:(j + 1) * 128], t_x[blk], lambda c, o0=o0: hT1[:, c, o0:o0 + 128], t_h1,
                          modA_col(0, r), shift(0, r))
        fw.barrier()
        sn.close()
        mix1 = xh["b"][:].bitcast(BF16)
        t_mix1 = T("mix1")

        VCLS = _na_valid_classes()

        def na_phase():
            ns = ExitStack()
            qT1 = fw.sbuf("qT1", [128, 1024], BF16, ns)
            kT1 = fw.sbuf("kT1", [128, 1536], BF16, ns)
            v1 = fw.sbuf("v1", [128, 12, 128], BF16, ns)
            nabt = fw.sbuf("nabt", [128, 15, 64], F32, ns)
            EB = fw.sbuf("EB", [128, 15, 64], F32, ns)
            cmask = fw.sbuf("cmask", [128, 64], F32, ns)
            nval = fw.sbuf("nval", [128, 160], F32, ns)
            peA = fw.sbuf("peA", [128, 512], F32, ns)
            peB = fw.sbuf("peB", [128, 384], F32, ns)
            pT7s = [fw.sbuf(f"pT7_{i}", [128, 7, 128], BF16, ns) for i in range(2)]
            t_peA, t_peB = T("peA"), T("peB")
            t_pT7 = [[T(f"pT7_{i}_{a}") for a in range(2)] for i in range(2)]
            rz1 = fw.sbuf("rz1", [128, 128], F32, ns)
            t_q1, t_k1, t_v1, t_nab, t_EB, t_cm, t_rz1 = (T(n) for n in "q1 k1 v1 nab EB cm rz1".split())
            t_pe = [T("pe0"), T("pe1"), T("pe2")]
            t_pTn = [T("pTn0"), T("pTn1"), T("pTn2"), T("pTn3")]
            t_pTq = [[T(f"pTq{i}_{a}") for a in range(2)] for i in range(4)]
            l_nab = fw.lane("nab")
            vT1 = nabt[:].rearrange("p a b -> p (a b)").bitcast(BF16)
            if os.environ.get("SBUF_REPORT"):
                print("SBUF remaining in NA phase:", nc.sbuf_bytes_remaining)
            fw.dma("sync", cmask[:], I.nacm, l_c, writes=[t_cm])
            fw.dma("sync", nval[:], I.naval, l_c, writes=[t_cm])
            specs = []
            for h in range(16):
                for col0 in (h * 128, 2048 + h * 128, 4096 + h * 128):
                    specs.append((I.win1[:, col0:col0 + 128].rearrange("(c p) n -> p c n", p=128), 16, 128))
            ws = WStream(specs)
            SC = 128 ** -0.5
            pk = 0
            pi = 0
            set_pools(4)
            for h in range(16):
                fw.dma("sync", nabt[:], I.nab[h], l_nab, writes=[t_nab])
                act(EB[:], nabt[:], AF.Exp, [t_nab], [t_EB])
                vtt(EB[:], EB[:], cmask[:].unsqueeze(1).to_broadcast([128, 15, 64]), ALU.mult, [t_EB, t_cm], [t_EB])
                w, tw = ws.get()
                fm_proj(w, tw, [(0, 512), (512, 512)],
                        lambda pb, tp, t0, n: act(qT1[:, t0:t0 + n], pb[:, 0:n], AF.Copy, [tp], [t_q1]), hT1, t_h1)
                ws.done()
                w, tw = ws.get()
                fm_proj(w, tw, [(0, 512), (512, 512), (1024, 512)],
                        lambda pb, tp, t0, n: act(kT1[:, t0:t0 + n], pb[:, 0:n], AF.Copy, [tp], [t_k1]), hT1, t_h1)
                ws.done()
                w, tw = ws.get()
                fm_proj(w, tw, [(0, 512), (512, 512), (1024, 512)],
                        lambda pb, tp, t0, n: act(vT1[:, t0:t0 + n], pb[:, 0:n], AF.Copy, [tp], [t_nab]), hT1, t_h1)
                ws.done()
                for q4 in range(3):
                    pb, tp = pget()
                    for i in range(4):
                        tt = 4 * q4 + i
                        mm(pb[:, i * 128:(i + 1) * 128], vT1[:, tt * 128:(tt + 1) * 128], identb[:], True, True,
                           [t_nab, t_const], [tp], signal=(i == 3))
                    vcopy(v1[:, 4 * q4:4 * q4 + 4, :], pb[:].rearrange("p (a b) -> p a b", b=128), [tp], [t_v1])
                def stage1(i):
                    qc = slice(i * 128, (i + 1) * 128)
                    j0 = max(i - 2, 0)
                    bi = i % 2
                    pT7, tq7 = pT7s[bi], t_pT7[bi]
                    psA, tpsA = pget()
                    for sl in range(4):
                        j = j0 + sl
                        mm(psA[:, sl * 128:(sl + 1) * 128], kT1[:, j * 128:(j + 1) * 128], qT1[:, qc], True, True,
                           [t_k1, t_q1], [tpsA], signal=(sl == 3))
                    psB, tpsB = pget()
                    for k3, j in enumerate((j0 + 4, 10, 11)):
                        mm(psB[:, k3 * 128:(k3 + 1) * 128], kT1[:, j * 128:(j + 1) * 128], qT1[:, qc], True, True,
                           [t_k1, t_q1], [tpsB], signal=(k3 == 2))
                    act(peA[:], psA[:], AF.Exp, [tpsA], [t_peA], scale=SC)
                    act(peB[:], psB[:, 0:384], AF.Exp, [tpsB], [t_peB], scale=SC)
                    for sl in range(5):
                        j = j0 + sl
                        src = peA[:, sl * 128:(sl + 1) * 128] if sl < 4 else peB[:, 0:128]
                        t_src = t_peA if sl < 4 else t_peB
                        for a in range(2):
                            for b_ in range(2):
                                dr = 2 * (j - i) + a - b_
                                idx = min(max(dr + 7, 0), 14)
                                vcol = ((i * 5 + sl) * 2 + a) * 2 + b_
                                pa_ = slice(64 * a, 64 * a + 64)
                                cb_ = slice(64 * b_, 64 * b_ + 64)
                                cls = VCLS[(i, sl, a, b_)]
                                dst = pT7[pa_, sl, cb_]
                                if cls == 2:
                                    en_ = "vector" if a == 0 else "gpsimd"
                                    vtt(dst, src[pa_, cb_], EB[pa_, idx, :], ALU.mult, [t_src, t_EB], [tq7[0 if en_ == "vector" else 1]],
                                        eng=en_, skip_self=True)
                                elif cls == 0:
                                    op("gpsimd", lambda en, o_=dst: en.memset(o_, 0.0), writes=[tq7[1]], skip_self=True)
                                else:
                                    vstt(dst, src[pa_, cb_], nval[pa_, vcol:vcol + 1], EB[pa_, idx, :],
                                         ALU.mult, ALU.mult, [t_src, t_cm, t_EB], [tq7[0]], skip_self=True)
                    vcopy(pT7[:, 5:7, :], peB[:, 128:384].rearrange("p (a b) -> p a b", b=128), [t_peB], [tq7[0]])

                def stage2(i):
                    qc = slice(i * 128, (i + 1) * 128)
                    j0 = max(i - 2, 0)
                    bi = i % 2
                    pT7, tq7 = pT7s[bi], t_pT7[bi]
                    PO, tPO = pget(1)
                    PZ, tPZ = pget(1)
                    keys = [j0 + sl for sl in range(5)] + [10, 11]
                    for k7, j in enumerate(keys):
                        first, last = k7 == 0, k7 == 6
                        mm(PO[:, 0:128], v1[:, j, :], pT7[:, k7, :], first, last, [t_v1] + tq7, [tPO])
                        mm(PZ[:, 0:128], onesb[:], pT7[:, k7, :], first, last, [t_const] + tq7, [tPZ])
                    op("vector", lambda en, PZ=PZ: en.reciprocal(out=rz1[:], in_=PZ[:, 0:128]), reads=[tPZ], writes=[t_rz1])
                    vtt(mix1[:, h, qc], PO[:, 0:128], rz1[:], ALU.mult, [tPO, t_rz1], [t_mix1])

                stage1(0)
                for i in range(8):
                    if i + 1 < 8:
                        stage1(i + 1)
                    stage2(i)
            fw.barrier()
            set_pools(6)
            ns.close()

        na_phase()
        if debug:
            fw.dma("sync", dbg["mix1"], mix1, l_c, reads=[t_mix1])
        if stop_after == "na":
            fw.barrier()
            fw.emit()
            return nc
        s1.close()

        def outproj1():
            specs = [(I.wout1[:, dc * 128:(dc + 1) * 128].rearrange("(c p) n -> p c n", p=128), 16, 128) for dc in range(16)]
            ws = WStream(specs)
            for dc in range(16):
                w, tw = ws.get()
                for blk in range(2):
                    xb, n = xblk(blk)
                    pb, tp = pget()
                    for mc in range(16):
                        mm(pb[:, 0:n], w[:, mc, :], mix1[:, mc, OWN0[blk]:OWN0[blk] + n], mc == 0, mc == 15, [tw, t_mix1], [tp])
                    vstt(xb[:, dc, :], pb[:, 0:n], gate(0, 0)(dc), xb[:, dc, :], ALU.mult, ALU.add,
                         [tp, t_mod, t_x[blk]], [t_x[blk]])
                ws.done()
            fw.barrier()

        outproj1()
        dbg_x("x3", 2)
        ffn(1, [0, 1])

        fs_ = ExitStack()
        alloc_norm_scratch(fs_, with_xst=False, nbuf=2)
        ofm2 = [fw.sbuf(f"ofm{i}", [128, 16, 128], F32, fs_) for i in range(2)]
        t_ofm2 = [T("ofm0"), T("ofm1")]
        otm = [fw.sbuf(f"otm{i}", [128, D], F32, fs_) for i in range(2)]
        zc = fw.sbuf("zc", [128, 1], F32, fs_)
        t_otm = [T("otm0"), T("otm1")]
        op("vector", lambda e: e.memset(zc[:], 0.0), writes=[t_const])
        for ot in range(8):
            xt, tx = xtile(ot)
            ofm, t_ofm = ofm2[ot % 2], t_ofm2[ot % 2]
            norm_tile(xt, tx, lambda c, ofm=ofm: ofm[:, c, :], t_ofm, lambda c: gfm[:, 4, c:c + 1], lambda c: zc[:, 0:1])
            o = otm[ot % 2]
            for q in range(4):
                pb, tp = pget()
                for jj in range(4):
                    c = 4 * q + jj
                    tr(pb[:, jj * 128:(jj + 1) * 128], ofm[:, c, :], ident, [t_ofm, t_const], [tp], signal=(jj == 3))
                act(o[:, q * 512:(q + 1) * 512], pb[:], AF.Copy, [tp], [t_otm[ot % 2]])
            fw.dma("sync", out_d[ot * 128:(ot + 1) * 128, :], o[:], l_io[ot % 4], reads=[t_otm[ot % 2]])
        fw.barrier()
        fw.emit()
        fs_.close()
    return nc


def _fm(v):
    return np.ascontiguousarray(np.asarray(v).reshape(-1, 128).T)


def _const_tables():
    r = np.arange(128)
    ident = (r[:, None] == r[None, :])
    le = (r[:, None] <= r[None, :])
    ge = (r[:, None] >= r[None, :])
    gt = (r[:, None] > r[None, :])
    lt = (r[:, None] < r[None, :])
    pm = np.zeros((128, 128), bool)
    for m in range(128):
        d = m % 32
        partner = m + 16 if d < 16 else m - 16
        pm[partner, m] = True
    return np.stack([ident, le, ge, gt, lt, pm], axis=1).astype(np.float32)


def _rope_tables(flip):
    t = np.arange(NLAT)
    tg = (NLAT - 1 - t) if flip else t
    row, col = tg // 64, tg % 64
    inv = (10000.0 ** (-np.arange(0, 32, 2, dtype=np.float32) / 32)).astype(np.float32)
    p = np.arange(128)
    d = p % 64
    i = d % 16
    pos = np.where((d < 32)[:, None], row[None, :], col[None, :]).astype(np.float32)
    ang = pos * inv[i][:, None]
    cosT = np.cos(ang).astype(np.float32)
    sgn = np.where((d % 32) < 16, -1.0, 1.0).astype(np.float32)
    sinS = (np.sin(ang) * sgn[:, None]).astype(np.float32)
    return cosT, sinS


def _na_row_valid(flip):
    R = (lambda r: 31 - r) if flip else (lambda r: r)
    val = np.zeros((8, 5, 2, 2), np.float32)
    for i in range(8):
        for s in range(5):
            j = max(i - 2, 0) + s
            for a in range(2):
                for b in range(2):
                    qr, kr = R(2 * i + b), R(2 * j + a)
                    rs = min(max(qr - 4, 0), 24)
                    val[i, s, a, b] = 1.0 if (0 <= kr < 32 and rs <= kr < rs + 8) else 0.0
    return val


def _na_valid_classes():
    v0, v1 = _na_row_valid(False), _na_row_valid(True)
    out = {}
    for i in range(8):
        for s in range(5):
            for a in range(2):
                for b in range(2):
                    x, y = v0[i, s, a, b], v1[i, s, a, b]
                    out[(i, s, a, b)] = 2 if (x == 1 and y == 1) else (0 if (x == 0 and y == 0) else 1)
    return out


def _na_tables(rpb, flip):
    sg = -1 if flip else 1
    kc = np.arange(64)[:, None]
    qc = np.arange(64)[None, :]
    dcl = kc - qc
    dci = np.clip(sg * dcl + 15, 0, 30)
    dri = np.clip(sg * (np.arange(15) - 7) + 7, 0, 14)
    g = rpb[:, dri][:, :, dci]
    g = np.transpose(g, (0, 2, 1, 3))
    g = np.concatenate([g, g], axis=1)
    C = (lambda c: 63 - c) if flip else (lambda c: c)
    qcg, kcg = C(qc), C(kc)
    cs = np.clip(qcg - 8, 0, 48)
    cm = ((kcg >= cs) & (kcg < cs + 16)).astype(np.float32)
    cm = np.concatenate([cm, cm], axis=0)
    R = (lambda r: 31 - r) if flip else (lambda r: r)
    val = np.zeros((8, 5, 2, 2), np.float32)
    for i in range(8):
        for s in range(5):
            j = max(i - 2, 0) + s
            for a in range(2):
                for b in range(2):
                    qr, kr = R(2 * i + b), R(2 * j + a)
                    rs = min(max(qr - 4, 0), 24)
                    val[i, s, a, b] = 1.0 if (0 <= kr < 32 and rs <= kr < rs + 8) else 0.0
    val = np.broadcast_to(val.reshape(1, 160), (128, 160)).copy()
    return np.ascontiguousarray(g.astype(np.float32)), cm, val


def host_prep(inputs):
    f = lambda k: np.asarray(inputs[k], dtype=np.float32)
    x, c, ctx, c_ctx = f("x"), f("c"), f("ctx"), f("c_ctx")
    ada_w, ada_b = f("ada_w"), f("ada_b")
    gfm = np.stack([_fm(f("norm_mix_g")[0]), _fm(f("norm_ffn_g")[0]), _fm(f("norm_mix_g")[1]),
                    _fm(f("norm_ffn_g")[1]), _fm(f("final_norm_g"))], axis=1)
    adab = np.stack([np.ascontiguousarray(ada_b[l].reshape(96, 128).T) for l in range(2)], axis=0)
    w_in = f("ev_w_in")[0]
    w_in_flip = w_in.copy()
    w_in_flip[:, 3072:3088] = w_in[:, 3088:3104]
    w_in_flip[:, 3088:3104] = w_in[:, 3072:3088]
    conv_w, conv_b = f("ev_conv_w")[0], f("ev_conv_b")[0]
    a_log, dt_bias, d_skip = f("ev_a_log")[0], f("ev_dt_bias")[0], f("ev_d_skip")[0]
    lamv = np.concatenate([f("ev_lam_q1")[0], f("ev_lam_k1")[0], f("ev_lam_q2")[0], f("ev_lam_k2")[0]])[None, :]
    consts = _const_tables()
    shared = dict(ada_w=ada_w, ada_b=adab, gfm=np.ascontiguousarray(gfm), ffn_w1=f("ffn_w1"), ffn_w3=f("ffn_w3"),
                  ffn_w2=f("ffn_w2"), convb=_fm(conv_b), ssmg=f("ev_ssm_norm_g")[0][None, :], lamv=lamv,
                  subg=f("ev_subln_g")[0][:, None].copy(), w_out0=f("ev_w_out")[0], w_in1=f("od_w_in")[0],
                  w_out1=f("od_w_out")[0], consts=consts)
    per_flip = {}
    for flip in (False, True):
        cw = conv_w[::-1] if flip else conv_w
        convw = np.ascontiguousarray(np.transpose(cw.reshape(5, 16, 128), (2, 1, 0)))
        dtb = dt_bias[::-1] if flip else dt_bias
        al = a_log[::-1] if flip else a_log
        vec = np.concatenate([dtb.reshape(-1), al.reshape(-1), d_skip])[None, :].astype(np.float32)
        cosT, sinS = _rope_tables(flip)
        nab, nacm, naval = _na_tables(f("od_rpb")[0], flip)
        per_flip[flip] = dict(w_in0=(w_in_flip if flip else w_in), convw=convw, vec=np.ascontiguousarray(vec),
                              cosT=cosT, sinS=sinS, na_bias=nab, na_cmask=nacm, na_valid=naval)
    in_maps = []
    c_ctx_fm = _fm(c_ctx)
    for cid in range(8):
        b, h = cid // 2, cid % 2
        flip = (h == 1)
        m = dict(shared)
        m.update(per_flip[flip])
        m["x"] = np.ascontiguousarray(x[b][::-1]) if flip else x[b]
        m["ctx"] = np.ascontiguousarray(ctx[b][::-1]) if flip else ctx[b]
        m["cond"] = np.ascontiguousarray(np.stack([_fm(c[b]), c_ctx_fm], axis=2))
        in_maps.append(m)
    return in_maps


_NC_CACHE = {}


def kernel(**inputs):
    in_maps = host_prep(inputs)
    if "nc" not in _NC_CACHE:
        _NC_CACHE["nc"] = build_program()
    nc = _NC_CACHE["nc"]
    res = run_bass_kernel_spmd(nc, in_maps, core_ids=list(range(8)))
    out = np.empty((4, NLAT, D), np.float32)
    for cid in range(8):
        b, h = cid // 2, cid % 2
        o = res.results[cid]["out"]
        if h == 0:
            out[b, 0:1024] = o
        else:
            out[b, 1024:2048] = o[::-1]
    return out
```
